# Optimizing a Trainium2 kernel written in Bass

```python
import math
import jax, jax.numpy as jnp
from jax import lax
import numpy as np

D_MODEL = 1024
BATCH = 4
SEQ = 8192
DEPTH = 4

CHUNK = 64
N_META = 16
Q_BLOCK = 128
HEAD_DIM = 64
DIFF_WIDTH = D_MODEL // 2
FOX_WIDTH = D_MODEL - DIFF_WIDTH
DIFF_HEADS = DIFF_WIDTH // (2 * HEAD_DIM)
FOX_HEADS = FOX_WIDTH // HEAD_DIM
ROT_DIM = HEAD_DIM // 4
ROPE_THETA = 500000.0
RMS_EPS = 1e-6
SUBLN_EPS = 1e-5
COL_SIZES = (DIFF_WIDTH, DIFF_WIDTH, DIFF_WIDTH, DIFF_WIDTH,
             FOX_WIDTH, FOX_WIDTH, FOX_WIDTH, FOX_HEADS, FOX_WIDTH)
D_IN = 4 * DIFF_WIDTH + 4 * FOX_WIDTH + FOX_HEADS

kernel_name = "hymba_diff_fox_streaming_trunk"


def rmsnorm(x, gain, eps=RMS_EPS):
    xf = x.astype(jnp.float32)
    y = xf * lax.rsqrt(jnp.mean(xf * xf, axis=-1, keepdims=True) + eps)
    return (y * gain.astype(jnp.float32)).astype(x.dtype)


def partial_rope(t, cos, sin):
    half = ROT_DIM // 2
    t1, t2, tp = t[..., :half], t[..., half:ROT_DIM], t[..., ROT_DIM:]
    cos = cos.astype(t.dtype)
    sin = sin.astype(t.dtype)
    return jnp.concatenate([t1 * cos - t2 * sin, t2 * cos + t1 * sin, tp], axis=-1)


def to_blocks(t, axis):
    shp = t.shape
    nb = shp[axis] // Q_BLOCK
    t = t.reshape(shp[:axis] + (nb, Q_BLOCK) + shp[axis + 1:])
    return jnp.moveaxis(t, axis, 0)


def from_blocks(o):
    o = jnp.moveaxis(o, 0, 2)
    b, h, nb, qb, d = o.shape
    return o.reshape(b, h, nb * qb, d)


def diff_attention(q, k, v, lam, chunk_ids):
    scale = HEAD_DIM ** -0.5
    q_blocks = to_blocks(q, 3)
    cq_blocks = chunk_ids.reshape(-1, Q_BLOCK)

    def step(blk):
        qb, cq = blk
        s = jnp.einsum('bhmqd,bhmkd->bhmqk', qb, k).astype(jnp.float32) * scale
        visible = chunk_ids[None, :] <= cq[:, None]
        s = jnp.where(visible, s, -jnp.inf)
        p = jax.nn.softmax(s, axis=-1)
        a = p[:, :, 0] - lam * p[:, :, 1]
        return jnp.einsum('bhqk,bhkd->bhqd', a.astype(v.dtype), v)

    return from_blocks(lax.map(step, (q_blocks, cq_blocks)))


def forgetting_attention(q, k, v, cum_logf, positions):
    scale = HEAD_DIM ** -0.5
    q_blocks = to_blocks(q, 2)
    c_blocks = jnp.moveaxis(cum_logf.reshape(cum_logf.shape[:2] + (-1, Q_BLOCK)), 2, 0)
    p_blocks = positions.reshape(-1, Q_BLOCK)

    def step(blk):
        qb, cq, pq = blk
        s = jnp.einsum('bhqd,bhkd->bhqk', qb, k).astype(jnp.float32) * scale
        s = s + cq[..., None] - cum_logf[:, :, None, :]
        visible = positions[None, :] <= pq[:, None]
        s = jnp.where(visible, s, -jnp.inf)
        p = jax.nn.softmax(s, axis=-1)
        return jnp.einsum('bhqk,bhkd->bhqd', p.astype(v.dtype), v)

    return from_blocks(lax.map(step, (q_blocks, c_blocks, p_blocks)))


def setup_inputs(seed: int = 0) -> dict:
    key = jax.random.key(seed)
    ks = jax.random.split(key, 12)
    f32 = jnp.float32
    return {
        "x": jax.random.normal(ks[0], (BATCH, SEQ, D_MODEL), f32),
        "meta_tokens": jax.random.normal(ks[1], (N_META, D_MODEL), f32),
        "norm_gain": 1.0 + 0.02 * jax.random.normal(ks[2], (DEPTH, D_MODEL), f32),
        "w_in": jax.random.normal(ks[3], (DEPTH, D_MODEL, D_IN), f32) * D_MODEL ** -0.5,
        "forget_bias": 0.1 * jax.random.normal(ks[4], (DEPTH, FOX_HEADS), f32),
        "lambda_q1": 0.1 * jax.random.normal(ks[5], (DEPTH, HEAD_DIM), f32),
        "lambda_k1": 0.1 * jax.random.normal(ks[6], (DEPTH, HEAD_DIM), f32),
        "lambda_q2": 0.1 * jax.random.normal(ks[7], (DEPTH, HEAD_DIM), f32),
        "lambda_k2": 0.1 * jax.random.normal(ks[8], (DEPTH, HEAD_DIM), f32),
        "subln_gain": 1.0 + 0.02 * jax.random.normal(ks[9], (DEPTH, 2 * HEAD_DIM), f32),
        "w_out": jax.random.normal(ks[10], (DEPTH, D_MODEL, D_MODEL), f32) * D_MODEL ** -0.5,
        "final_gain": 1.0 + 0.02 * jax.random.normal(ks[11], (D_MODEL,), f32),
    }


def reference(x, meta_tokens, norm_gain, w_in, forget_bias, lambda_q1, lambda_k1,
              lambda_q2, lambda_k2, subln_gain, w_out, final_gain):
    b, seq, d = x.shape
    total = N_META + seq
    lp = ((total + Q_BLOCK - 1) // Q_BLOCK) * Q_BLOCK
    meta = jnp.broadcast_to(meta_tokens.astype(x.dtype)[None], (b, N_META, d))
    h = jnp.concatenate([meta, x], axis=1)
    h = jnp.pad(h, ((0, 0), (0, lp - total), (0, 0)))

    positions = jnp.arange(lp, dtype=jnp.int32)
    chunk_ids = jnp.where(positions < N_META, 0, (positions - N_META) // CHUNK + 1)
    inv_freq = 1.0 / (ROPE_THETA ** (jnp.arange(0, ROT_DIM, 2, dtype=jnp.float32) / ROT_DIM))
    ang = positions.astype(jnp.float32)[:, None] * inv_freq[None, :]
    cos = jnp.cos(ang)[None, :, None, None, :]
    sin = jnp.sin(ang)[None, :, None, None, :]
    split_points = [int(v) for v in np.cumsum(COL_SIZES)[:-1]]

    for l in range(DEPTH):
        hn = rmsnorm(h, norm_gain[l])
        proj = hn @ w_in[l]
        dq, dk, dv, dg, fq, fk, fv, ff, fg = jnp.split(proj, split_points, axis=-1)

        dq = partial_rope(dq.reshape(b, lp, DIFF_HEADS, 2, HEAD_DIM), cos, sin)
        dk = partial_rope(dk.reshape(b, lp, DIFF_HEADS, 2, HEAD_DIM), cos, sin)
        dq = jnp.transpose(dq, (0, 2, 3, 1, 4))
        dk = jnp.transpose(dk, (0, 2, 3, 1, 4))
        dv = jnp.transpose(dv.reshape(b, lp, DIFF_HEADS, 2 * HEAD_DIM), (0, 2, 1, 3))
        lam_init = 0.8 - 0.6 * math.exp(-0.3 * l)
        lam = (jnp.exp(jnp.sum(lambda_q1[l].astype(jnp.float32) * lambda_k1[l].astype(jnp.float32)))
               - jnp.exp(jnp.sum(lambda_q2[l].astype(jnp.float32) * lambda_k2[l].astype(jnp.float32)))
               + lam_init)
        do = diff_attention(dq, dk, dv, lam, chunk_ids)
        do = rmsnorm(do, subln_gain[l], SUBLN_EPS) * (1.0 - lam_init)
        do = jnp.transpose(do, (0, 2, 1, 3)).reshape(b, lp, DIFF_WIDTH)
        do = do * jax.nn.silu(dg)

        fq = jnp.transpose(fq.reshape(b, lp, FOX_HEADS, HEAD_DIM), (0, 2, 1, 3))
        fk = jnp.transpose(fk.reshape(b, lp, FOX_HEADS, HEAD_DIM), (0, 2, 1, 3))
        fv = jnp.transpose(fv.reshape(b, lp, FOX_HEADS, HEAD_DIM), (0, 2, 1, 3))
        log_f = jax.nn.log_sigmoid(ff.astype(jnp.float32) + forget_bias[l].astype(jnp.float32))
        cum_logf = jnp.transpose(jnp.cumsum(log_f, axis=1), (0, 2, 1))
        fo = forgetting_attention(fq, fk, fv, cum_logf, positions)
        fo = jnp.transpose(fo, (0, 2, 1, 3)).reshape(b, lp, FOX_WIDTH)
        fo = fo * jax.nn.silu(fg)

        h = h + jnp.concatenate([do, fo], axis=-1) @ w_out[l]

    out = rmsnorm(h, final_gain)
    return out[:, N_META:N_META + seq]
```

```python
import math
import contextlib
import numpy as np
import ml_dtypes
import concourse.bass as bass
import concourse.mybir as mybir
from concourse.bass_utils import run_bass_kernel_spmd

F32 = mybir.dt.float32
BF16 = mybir.dt.bfloat16
AF = mybir.ActivationFunctionType
ALU = mybir.AluOpType
AX = mybir.AxisListType

D = 1024
NMETA = 16
DEPTH = 4
WCOLS = 2564
NEG = -30000.0
TICK_LIMIT = 30000
SAME_ENG_SYNC = True
DEBUG = False
LAG_STEPS = 2
NPASS = 8


class Buf:
    __slots__ = ("name", "writers", "readers", "prev", "sem", "semval")

    def __init__(self, name):
        self.name = name
        self.writers = []
        self.readers = []
        self.prev = []
        self.sem = None
        self.semval = 0


class Op:
    __slots__ = ("eng", "fn", "deps", "is_dma", "sem", "semval", "tick", "needs_inc", "pos", "name")

    def __init__(self, eng, fn, name=""):
        self.eng = eng
        self.fn = fn
        self.deps = []
        self.is_dma = False
        self.sem = None
        self.semval = 0
        self.tick = None
        self.needs_inc = False
        self.pos = None
        self.name = name


class Sched:
    ENGS = ("pe", "act", "dve", "pool", "sp")

    def __init__(self, nc, es):
        self.nc = nc
        self.es = es
        self.streams = {e: [] for e in self.ENGS}
        self.last = {e: None for e in self.ENGS}
        self.dma_ops = []
        self.barrier_deps = []
        self.need_barrier = {e: False for e in self.ENGS}
        self.nsem = 0
        self.all_dma_bufs = []

    def new_sem(self, name):
        self.nsem += 1
        return self.es.enter_context(self.nc.semaphore(f"s{self.nsem}_{name}"))

    def _read(self, op, b):
        op.deps.extend(b.writers)
        b.readers.append(op)

    def _write(self, op, b, partial):
        if b.readers:
            b.prev = [x for x in b.readers + b.writers if x is not op]
            b.writers = []
            b.readers = []
        elif not partial:
            b.prev = b.prev + b.writers
            b.writers = []
        op.deps.extend(b.prev)
        b.writers.append(op)

    def op(self, eng, fn, R=(), W=(), PW=(), dma=None, place=True, name=""):
        o = Op(eng, fn, name)
        if self.need_barrier[eng]:
            o.deps.extend(self.barrier_deps)
            self.need_barrier[eng] = False
        for b in R:
            self._read(o, b)
        for b in W:
            self._write(o, b, False)
        for b in PW:
            self._write(o, b, True)
        if dma is not None:
            o.is_dma = True
            if dma.sem is None:
                dma.sem = self.new_sem(dma.name)
                self.all_dma_bufs.append(dma)
            dma.semval += 16
            o.sem = dma.sem
            o.semval = dma.semval
            self.dma_ops.append(o)
        if place:
            self.place(o)
        return o

    def place(self, o):
        o.pos = len(self.streams[o.eng])
        self.streams[o.eng].append(o)
        self.last[o.eng] = o

    def seal(self, b):
        b.prev = b.prev + b.writers
        b.writers = []

    def barrier(self):
        deps = [o for o in self.last.values() if o is not None]
        best = {}
        for o in self.dma_ops:
            k = id(o.sem)
            if k not in best or best[k].semval < o.semval:
                best[k] = o
        deps.extend(best.values())
        self.barrier_deps = deps
        self.dma_ops = list(best.values())
        for e in self.ENGS:
            self.need_barrier[e] = True

    def final_wait(self):
        self.barrier()
        self.op("sp", None, name="final")

    def finalize(self):
        for e in self.ENGS:
            for o in self.streams[e]:
                for d in o.deps:
                    if d.pos is None:
                        raise RuntimeError(f"dep {d.name} of {o.name} was never placed")
                    if d.is_dma:
                        continue
                    if d.eng != o.eng or (SAME_ENG_SYNC and o.eng != 'pe') or o.is_dma:
                        d.needs_inc = True
                    elif d.pos >= o.pos:
                        raise RuntimeError(f"same-engine dep order violated {d.name} -> {o.name}")
        self.tick_sems = {}
        for e in self.ENGS:
            n = 0
            sems = []
            for o in self.streams[e]:
                if o.needs_inc and not o.is_dma:
                    k = n // TICK_LIMIT
                    if k >= len(sems):
                        sems.append(self.new_sem(f"tick_{e}{k}"))
                    o.sem = sems[k]
                    o.tick = n % TICK_LIMIT + 1
                    n += 1
            self.tick_sems[e] = sems
        ptr = {e: 0 for e in self.ENGS}
        done = set()
        total = sum(len(s) for s in self.streams.values())
        ndone = 0
        while ndone < total:
            prog = False
            for e in self.ENGS:
                s = self.streams[e]
                while ptr[e] < len(s):
                    o = s[ptr[e]]
                    if all((id(d) in done) for d in o.deps):
                        done.add(id(o))
                        ptr[e] += 1
                        ndone += 1
                        prog = True
                    else:
                        break
            if not prog:
                msg = []
                for e in self.ENGS:
                    if ptr[e] < len(self.streams[e]):
                        o = self.streams[e][ptr[e]]
                        bad = [d.name for d in o.deps if id(d) not in done]
                        msg.append(f"{e}: {o.name} waits {bad[:4]}")
                raise RuntimeError("DEADLOCK in schedule: " + " | ".join(msg))

    def emit(self, eng, e):
        waited = {}
        nwait = 0
        for o in self.streams[eng]:
            waits = {}
            for d in o.deps:
                if d.is_dma:
                    key = id(d.sem)
                    v = d.semval
                elif d.eng != eng or (SAME_ENG_SYNC and eng != 'pe') or o.is_dma:
                    key = id(d.sem)
                    v = d.tick
                else:
                    continue
                if key not in waits or waits[key][1] < v:
                    waits[key] = (d.sem, v)
            for key, (sem, v) in waits.items():
                if waited.get(key, 0) < v:
                    e.wait_ge(sem, v)
                    waited[key] = v
                    nwait += 1
            if o.fn is None:
                continue
            ins = o.fn(e)
            if o.is_dma:
                ins.then_inc(o.sem, 16)
            elif o.needs_inc:
                ins.then_inc(o.sem, 1)
        return nwait

    def run(self):
        self.finalize()
        with self.nc.Block() as block:
            @block.tensor
            def _(e):
                self.emit("pe", e)

            @block.scalar
            def _(e):
                self.emit("act", e)

            @block.vector
            def _(e):
                self.emit("dve", e)

            @block.gpsimd
            def _(e):
                self.emit("pool", e)

            @block.sync
            def _(e):
                self.emit("sp", e)


class Arena:
    def __init__(self, ap2d, nelem):
        self.ap = ap2d
        self.n = nelem
        self.off = 0

    def reset(self):
        self.off = 0

    def alloc(self, shape, dtype, parts=128):
        n = 1
        for s in shape:
            n *= s
        ne = n * (2 if dtype == F32 else 1)
        ne = (ne + 1) // 2 * 2
        if self.off + ne > self.n:
            raise RuntimeError(f"arena overflow: need {self.off + ne} > {self.n}")
        v = self.ap[:, self.off:self.off + ne]
        self.off += ne
        if dtype == F32:
            v = v.bitcast(F32)
        if n * (2 if dtype == F32 else 1) != ne:
            v = v[:, 0:n]
        if len(shape) == 2:
            v = v.rearrange("p (a b) -> p a b", b=shape[1])
        elif len(shape) == 3:
            v = v.rearrange("p (a b c) -> p a b c", b=shape[1], c=shape[2])
        if parts < 128:
            v = v[0:parts]
        return v


class Builder:
    def __init__(self, NT, mode):
        self.NT = NT
        self.LP = NT * 128
        self.mode = mode
        self.nc = bass.Bass("TRN2", target_bir_lowering=False)
        self.es = contextlib.ExitStack()
        self.S = Sched(self.nc, self.es)
        self.nbuf = 0

    def B(self, name):
        self.nbuf += 1
        return Buf(f"{name}{self.nbuf}")

    def dram_in(self, name, shape, dt):
        return self.nc.dram_tensor(name, list(shape), dt, kind="ExternalInput").ap()

    def dram_out(self, name, shape, dt):
        return self.nc.dram_tensor(name, list(shape), dt, kind="ExternalOutput").ap()

    def dram_tmp(self, name, shape, dt):
        if DEBUG:
            return self.nc.dram_tensor(name, list(shape), dt, kind="ExternalOutput").ap()
        return self.nc.dram_tensor(name, list(shape), dt).ap()

    def sb(self, name, shape, dt):
        return self.es.enter_context(self.nc.sbuf_tensor(name, list(shape), dt))

    def ps(self, name, shape, dt):
        return self.es.enter_context(self.nc.psum_tensor(name, list(shape), dt))

    def dbg(self, name, ap, buf, shape, dt):
        if not DEBUG:
            return
        if not hasattr(self, "_dbg"):
            self._dbg = []
        self._dbg.append((name, ap, buf, shape, dt))

    def dbg_flush(self):
        if not DEBUG or not hasattr(self, "_dbg"):
            return
        for name, ap, buf, shape, dt in self._dbg:
            d = self.nc.dram_tensor("dbg_" + name, list(shape), dt, kind="ExternalOutput").ap()
            self.dma("sp", d, ap, buf, R=[buf], name="dbg_" + name)

    def dma(self, q, out, in_, sbuf, R=(), W=(), PW=(), name="dma"):
        return self.S.op(q, lambda e, o=out, i=in_: e.dma_start(out=o, in_=i), R=R, W=W, PW=PW, dma=sbuf, name=name)

    def mm(self, out, lhsT, rhs, start, stop, R=(), W=(), PW=(), place=True, name="mm"):
        return self.S.op("pe", lambda e, o=out, l=lhsT, r=rhs, s0=start, s1=stop: e.matmul(o, lhsT=l, rhs=r, start=s0, stop=s1),
                         R=R, W=W, PW=PW, place=place, name=name)

    def tr(self, out, in_, ident, R=(), W=(), PW=(), name="tr"):
        return self.S.op("pe", lambda e, o=out, i=in_, d=ident: e.transpose(out=o, in_=i, identity=d), R=R, W=W, PW=PW, name=name)

    def act(self, out, in_, func, R=(), W=(), PW=(), bias=None, scale=None, accum=None, name="act"):
        kw = {}
        if bias is not None:
            kw["bias"] = bias
        if scale is not None:
            kw["scale"] = scale
        if accum is not None:
            kw["accum_out"] = accum
        return self.S.op("act", lambda e, o=out, i=in_, f=func, k=kw: e.activation(out=o, in_=i, func=f, **k), R=R, W=W, PW=PW, name=name)

    def cp(self, eng, out, in_, R=(), W=(), PW=(), name="cp"):
        return self.S.op(eng, lambda e, o=out, i=in_: e.tensor_copy(out=o, in_=i), R=R, W=W, PW=PW, name=name)

    def tt(self, eng, out, a, b, op, R=(), W=(), PW=(), name="tt"):
        return self.S.op(eng, lambda e, o=out, x=a, y=b, p=op: e.tensor_tensor(out=o, in0=x, in1=y, op=p), R=R, W=W, PW=PW, name=name)

    def ts(self, eng, out, a, s1, s2, op0, op1=None, R=(), W=(), PW=(), name="ts"):
        if op1 is None:
            return self.S.op(eng, lambda e, o=out, x=a, c1=s1, p0=op0: e.tensor_single_scalar(out=o, in_=x, scalar=c1, op=p0), R=R, W=W, PW=PW, name=name)
        return self.S.op(eng, lambda e, o=out, x=a, c1=s1, c2=s2, p0=op0, p1=op1: e.tensor_scalar(out=o, in0=x, scalar1=c1, scalar2=c2, op0=p0, op1=p1),
                         R=R, W=W, PW=PW, name=name)

    def stt(self, eng, out, a, s, b, op0, op1, R=(), W=(), PW=(), name="stt"):
        return self.S.op(eng, lambda e, o=out, x=a, c=s, y=b, p0=op0, p1=op1: e.scalar_tensor_tensor(out=o, in0=x, scalar=c, in1=y, op0=p0, op1=p1),
                         R=R, W=W, PW=PW, name=name)

    def memset(self, eng, ap, val, W=(), PW=(), name="memset"):
        return self.S.op(eng, lambda e, o=ap, v=val: e.memset(o, v), W=W, PW=PW, name=name)

    def build(self):
        NT, LP, mode = self.NT, self.LP, self.mode
        NCH = (NT + 3) // 4
        has_op = mode in ("mid", "final")
        has_layer = mode in ("first", "mid")

        h_in = self.dram_in("h_in", [LP, D], F32)
        ident_d = self.dram_in("ident", [128, 128], BF16)
        gain_d = self.dram_in("gain", [128, D], F32)
        if has_op:
            gcat = self.dram_in("gcat", [LP, D], BF16)
            wo_d = self.dram_in("wo", [D, D], F32)
        if mode == "mid":
            h_out = self.dram_out("h_out", [LP, D], F32)
        if mode == "final":
            y_out = self.dram_out("y_out", [LP, D], F32)
        if has_layer:
            wi_d = self.dram_in("wi", [D, WCOLS], F32)
            fb_d = self.dram_in("fb", [4, 1], F32)
            tab_d = self.dram_in("tab", [16, 4, LP], F32)
            lam_d = self.dram_in("lam4", [128, 4, 64], F32)
            sg_d = self.dram_in("sg", [128, 128], F32)
            li_d = self.dram_in("linit", [128, 2], F32)
            maskF_d = self.dram_in("maskF", [128, 128], BF16)
            maskD_d = self.dram_in("maskD", [128, 128], BF16)
            maskP_d = self.dram_in("maskP", [16, 512], BF16)
            g_out = self.dram_out("g_out", [LP, 512], BF16)
            QTs = self.dram_tmp("QTs", [512, LP], BF16)
            KTs = self.dram_tmp("KTs", [512, LP], BF16)
            Vs = self.dram_tmp("Vs", [128, NT, 512], BF16)
            Gs = self.dram_tmp("Gs", [128, NT, 512], BF16)
            CQs = self.dram_tmp("CQs", [4, 3, LP], BF16)
            CKs = self.dram_tmp("CKs", [4, 3, LP], BF16)
            bQTs, bKTs, bVs, bGs, bCQs, bCKs = (self.B(n) for n in ("QTs", "KTs", "Vs", "Gs", "CQs", "CKs"))

        ps_all = self.ps("ps_all", [128, 8, 512], F32)
        b_bank = [self.B(f"bank{i}") for i in range(8)]

        ident = self.sb("ident_sb", [128, 128], BF16)
        b_ident = self.B("ident")
        self.dma("sp", ident[:], ident_d[:, :], b_ident, W=[b_ident], name="ld_ident")
        eps6 = self.sb("eps6", [128, 1], F32)
        eps5 = self.sb("eps5", [128, 1], F32)
        b_eps = self.B("eps")
        self.memset("pool", eps6[:], 1e-6, PW=[b_eps])
        self.memset("pool", eps5[:], 1e-5, PW=[b_eps])
        gain = self.sb("gain_sb", [128, D], F32)
        b_gain = self.B("gain")
        self.dma("sp", gain[:], gain_d[:, :], b_gain, W=[b_gain], name="ld_gain")
        if has_op:
            Wo = self.sb("Wo", [128, 8, D], BF16)
            b_Wo = self.B("Wo")
        if has_layer:
            Wi = self.sb("Wi", [128, 8, WCOLS], BF16)
            b_Wi = self.B("Wi")
        wst = [self.sb(f"wst{i}", [128, WCOLS], F32) for i in range(2)]
        b_wst = [self.B("wst") for _ in range(2)]
        ARENA_N = 59000 if has_layer else 24000
        arena_t = self.sb("arena", [128, ARENA_N], BF16)
        AR = Arena(arena_t, ARENA_N)

        k = 0
        cast_engs = ["dve", "pool"]
        if has_op:
            for d in range(8):
                s = k % 2
                self.dma("sp", wst[s][:, 0:D], wo_d[d * 128:(d + 1) * 128, :], b_wst[s], W=[b_wst[s]], name="ld_wo")
                self.cp(cast_engs[k % 2], Wo[:, d, :], wst[s][:, 0:D], R=[b_wst[s]], PW=[b_Wo], name="cast_wo")
                k += 1
        if has_layer:
            for d in range(8):
                s = k % 2
                self.dma("sp", wst[s][:, :], wi_d[d * 128:(d + 1) * 128, :], b_wst[s], W=[b_wst[s]], name="ld_wi")
                self.cp(cast_engs[k % 2], Wi[:, d, :], wst[s][:, :], R=[b_wst[s]], PW=[b_Wi], name="cast_wi")
                k += 1

        if has_layer:
            fbn = self.sb("fbn", [4, 1], F32)
            b_fbn = self.B("fbn")
            self.dma("sp", fbn[:], fb_d[:, :], b_fbn, W=[b_fbn], name="ld_fb")
            self.ts("dve", fbn[:], fbn[:], -1.0, None, ALU.mult, R=[b_fbn], W=[b_fbn], name="negfb")
            lam4 = self.sb("lam4_sb", [128, 4, 64], F32)
            b_lam4 = self.B("lam4")
            self.dma("sp", lam4[:], lam_d[:, :, :], b_lam4, W=[b_lam4], name="ld_lam")
            sg = self.sb("sg_sb", [128, 128], F32)
            b_sg = self.B("sg")
            self.dma("sp", sg[:], sg_d[:, :], b_sg, W=[b_sg], name="ld_sg")
            li = self.sb("li_sb", [128, 2], F32)
            b_li = self.B("li")
            self.dma("sp", li[:], li_d[:, :], b_li, W=[b_li], name="ld_li")
            lamw = self.sb("lamw", [128, 8], F32)
            b_lamw = self.B("lamw")
            lprod = self.sb("lprod", [128, 2, 64], F32)
            b_lprod = self.B("lprod")
            self.tt("dve", lprod[:, 0, :], lam4[:, 0, :], lam4[:, 1, :], ALU.mult, R=[b_lam4], PW=[b_lprod])
            self.tt("dve", lprod[:, 1, :], lam4[:, 2, :], lam4[:, 3, :], ALU.mult, R=[b_lam4], PW=[b_lprod])
            self.S.op("dve", lambda e: e.reduce_sum(out=lamw[:, 0:2], in_=lprod[:], axis=AX.X), R=[b_lprod], W=[b_lamw], name="lam_red")
            self.act(lamw[:, 2:4], lamw[:, 0:2], AF.Exp, R=[b_lamw], W=[b_lamw], name="lam_exp")
            self.tt("dve", lamw[:, 4:5], lamw[:, 2:3], lamw[:, 3:4], ALU.subtract, R=[b_lamw], W=[b_lamw])
            self.tt("dve", lamw[:, 4:5], lamw[:, 4:5], li[:, 0:1], ALU.add, R=[b_lamw, b_li], W=[b_lamw])
            self.ts("dve", lamw[:, 5:6], lamw[:, 4:5], -1.0, None, ALU.mult, R=[b_lamw], W=[b_lamw], name="neglam")
            neglam = lamw[:, 5:6]
            self.ts("dve", sg[:], sg[:], li[:, 1:2], None, ALU.mult, R=[b_sg, b_li], W=[b_sg], name="sg2")
            maskF = self.sb("maskF_sb", [128, 128], BF16)
            maskD = self.sb("maskD_sb", [128, 128], BF16)
            maskP = self.sb("maskP_sb", [16, 512], BF16)
            b_mask = self.B("mask")
            self.dma("sp", maskF[:], maskF_d[:, :], b_mask, PW=[b_mask])
            self.dma("sp", maskD[:], maskD_d[:, :], b_mask, PW=[b_mask])
            self.dma("sp", maskP[:], maskP_d[:, :], b_mask, PW=[b_mask])

        AR.reset()
        hT = [AR.alloc([D], F32) for _ in range(3)]
        b_hT = [self.B("hT") for _ in range(3)]
        if has_op:
            gt = [AR.alloc([D], BF16) for _ in range(2)]
            b_gt = [self.B("gt") for _ in range(2)]
            gTs = [AR.alloc([8, 128], BF16) for _ in range(2)]
            b_gTs = [self.B("gTs") for _ in range(2)]
        hn = [AR.alloc([D], BF16) for _ in range(2)]
        b_hn = [self.B("hn") for _ in range(2)]
        sq = AR.alloc([D], BF16)
        b_sq = self.B("sq")
        stat = AR.alloc([3, 8], F32)
        b_stat = [self.B("stat") for _ in range(3)]
        if mode == "final":
            yst = [AR.alloc([D], F32) for _ in range(2)]
            b_yst = [self.B("yst") for _ in range(2)]
        if has_layer:
            hnT = [AR.alloc([8, 512], BF16) for _ in range(2)]
            b_hnT = [self.B("hnT") for _ in range(2)]
            tabs = [AR.alloc([4, 512], F32) for _ in range(2)]
            b_tabs = [self.B("tabs") for _ in range(2)]
            t1 = [AR.alloc([512], F32) for _ in range(2)]
            t2 = [AR.alloc([512], F32) for _ in range(2)]
            b_t1 = [self.B("t1") for _ in range(2)]
            b_t2 = [self.B("t2") for _ in range(2)]
            fst = [AR.alloc([512], BF16) for _ in range(4)]
            b_fst = [self.B("fst") for _ in range(4)]
            vst = [AR.alloc([512], BF16) for _ in range(2)]
            b_vst = [self.B("vst") for _ in range(2)]
            gst = [AR.alloc([512], BF16) for _ in range(2)]
            b_gst = [self.B("gst") for _ in range(2)]
            ffe = AR.alloc([512], F32, parts=4)
            ffn = AR.alloc([512], F32, parts=4)
            cneg = [AR.alloc([512], F32, parts=4) for _ in range(2)]
            fft = AR.alloc([512], F32, parts=4)
            ffr1 = AR.alloc([512], F32, parts=4)
            ffr2 = AR.alloc([512], F32, parts=4)
            ones4 = AR.alloc([512], F32, parts=4)
            pk = [AR.alloc([3, 512], BF16, parts=4) for _ in range(2)]
            pq = [AR.alloc([3, 512], BF16, parts=4) for _ in range(2)]
            b_ffe, b_ffn, b_fft, b_ffr1, b_ffr2, b_ones4 = (self.B(n) for n in ("ffe", "ffn", "fft", "ffr1", "ffr2", "ones4"))
            b_cneg = [self.B("cneg") for _ in range(2)]
            b_pk = [self.B("pk") for _ in range(2)]
            b_pq = [self.B("pq") for _ in range(2)]
            self.memset("pool", ones4, 1.0, W=[b_ones4])
            for i in range(2):
                self.memset("pool", tabs[i][:, 0, :], 0.125, PW=[b_tabs[i]])
                self.memset("pool", tabs[i][:, 1, :], 0.0, PW=[b_tabs[i]])
                self.memset("pool", tabs[i][:, 2, :], 1.0, PW=[b_tabs[i]])
                self.memset("pool", tabs[i][:, 3, :], 0.0, PW=[b_tabs[i]])
                self.S.seal(b_tabs[i])

        ps_tp = [ps_all[:, i, :].bitcast(BF16).rearrange("p (a b) -> p a b", b=128) for i in range(2)]
        b_tp = b_bank[0:2]
        if has_op:
            ps_op = ps_all[:, 2:4, :]
            b_op = b_bank[2]
            mm_banks = [4, 5, 6, 7]
        else:
            mm_banks = [2, 3, 4, 5, 6, 7]
        mmi = [0]

        def next_mm():
            i = mm_banks[mmi[0] % len(mm_banks)]
            mmi[0] += 1
            return ps_all[:, i, :], b_bank[i]

        tpi = 0
        carry = None
        for c in range(NCH):
            t0 = 4 * c
            nt = min(4, NT - t0)
            cw = nt * 128
            cs = slice(t0 * 128, t0 * 128 + cw)
            if has_layer:
                hb = c % 2
                tb = c % 2
                self.dma("sp", tabs[tb][0:16, :, 0:cw], tab_d[:, :, cs], b_tabs[tb], PW=[b_tabs[tb]], name="ld_tab")
                self.dma("sp", tabs[tb][64:80, :, 0:cw], tab_d[:, :, cs], b_tabs[tb], PW=[b_tabs[tb]], name="ld_tab2")
            for tt_ in range(nt):
                t = t0 + tt_
                hs = t % 3
                rows = slice(t * 128, (t + 1) * 128)
                self.dma("sp", hT[hs], h_in[rows, :], b_hT[hs], W=[b_hT[hs]], name=f"ld_h{t}")
                if has_op:
                    gs_ = t % 2
                    self.dma("sp", gt[gs_], gcat[rows, :], b_gt[gs_], W=[b_gt[gs_]], name=f"ld_g{t}")
                    tp, btp = ps_tp[tpi % 2], b_tp[tpi % 2]
                    tpi += 1
                    for d in range(8):
                        self.tr(tp[:, d, :], gt[gs_][:, d * 128:(d + 1) * 128], ident[:], R=[b_gt[gs_], b_ident], PW=[btp], name="trG")
                    self.act(gTs[gs_], tp[:, :, :], AF.Copy, R=[btp], W=[b_gTs[gs_]], name="cpGT")
                    for half in range(2):
                        for d in range(8):
                            self.mm(ps_op[:, half, :], gTs[gs_][:, d, :], Wo[:, d, half * 512:(half + 1) * 512], d == 0, d == 7,
                                    R=[b_gTs[gs_], b_Wo], PW=[b_op], name="mm_op")
                    self.tt("dve", hT[hs], hT[hs], ps_op.rearrange("p a b -> p (a b)"), ALU.add, R=[b_op, b_hT[hs]], W=[b_hT[hs]], name="resid")
                    if mode == "mid":
                        self.dma("act", h_out[rows, :], hT[hs], b_hT[hs], R=[b_hT[hs]], name=f"st_h{t}")
                st = stat[:, t % 3, :]
                bst = b_stat[t % 3]
                self.act(sq, hT[hs], AF.Square, R=[b_hT[hs]], W=[b_sq, bst], accum=st[:, 0:1], name="sqsum")
                self.act(st[:, 2:3], st[:, 0:1], AF.Sqrt, R=[bst, b_eps], W=[bst], bias=eps6[:, 0:1], scale=1.0 / D, name="ms")
                self.S.op("dve", lambda e, o=st[:, 1:2], i=st[:, 2:3]: e.reciprocal(out=o, in_=i), R=[bst], W=[bst], name="rstd")
                if mode == "final":
                    ys = t % 2
                    self.stt("dve", yst[ys], hT[hs], st[:, 1:2], gain[:], ALU.mult, ALU.mult, R=[b_hT[hs], bst, b_gain], W=[b_yst[ys]], name="ynorm")
                    self.dma("act", y_out[rows, :], yst[ys], b_yst[ys], R=[b_yst[ys]], name=f"st_y{t}")
                    continue
                hs2 = t % 2
                self.stt("dve", hn[hs2], hT[hs], st[:, 1:2], gain[:], ALU.mult, ALU.mult, R=[b_hT[hs], bst, b_gain], W=[b_hn[hs2]], name="hn")
                tp, btp = ps_tp[tpi % 2], b_tp[tpi % 2]
                tpi += 1
                for d in range(8):
                    self.tr(tp[:, d, :], hn[hs2][:, d * 128:(d + 1) * 128], ident[:], R=[b_hn[hs2], b_ident], PW=[btp], name="trH")
                self.act(hnT[hb][:, :, tt_ * 128:(tt_ + 1) * 128], tp[:, :, :], AF.Copy, R=[btp], PW=[b_hnT[hb]], name="cpHT")
                pv, bpv = next_mm()
                for d in range(8):
                    self.mm(pv[:, :], hnT[hb][:, d, tt_ * 128:(tt_ + 1) * 128], Wi[:, d, 1540:2052], d == 0, d == 7,
                            R=[b_hnT[hb], b_Wi], PW=[bpv], name="mm_v")
                vs_ = t % 2
                self.cp("dve", vst[vs_], pv[:, :], R=[bpv], W=[b_vst[vs_]], name="cp_v")
                self.dma("act", Vs[:, t, :], vst[vs_], b_vst[vs_], R=[b_vst[vs_]], PW=[bVs], name="st_v")
                pg, bpg = next_mm()
                for d in range(8):
                    self.mm(pg[:, :], hnT[hb][:, d, tt_ * 128:(tt_ + 1) * 128], Wi[:, d, 2052:2564], d == 0, d == 7,
                            R=[b_hnT[hb], b_Wi], PW=[bpg], name="mm_g")
                self.act(gst[vs_], pg[:, :], AF.Silu, R=[bpg], W=[b_gst[vs_]], name="silu")
                self.dma("act", Gs[:, t, :], gst[vs_], b_gst[vs_], R=[b_gst[vs_]], PW=[bGs], name="st_g")
            if not has_layer:
                continue

            def fm_block(blk):
                p, bp = next_mm()
                for d in range(8):
                    self.mm(p[:, 0:cw], Wi[:, d, blk * 128:(blk + 1) * 128], hnT[hb][:, d, 0:cw], d == 0, d == 7,
                            R=[b_hnT[hb], b_Wi], PW=[bp], name=f"mm_fm{blk}")
                return p, bp

            fi = [0]

            def store_fm(src_fn, dst, bdst):
                s = fi[0] % 4
                fi[0] += 1
                src_fn(fst[s], b_fst[s])
                self.dma("act", dst, fst[s][:, 0:cw], b_fst[s], R=[b_fst[s]], PW=[bdst], name="st_fm")

            rope = [(0, 2, 0, 1, QTs, bQTs, 0), (1, 3, 0, 1, QTs, bQTs, 128), (4, 6, 2, 3, KTs, bKTs, 0), (5, 7, 2, 3, KTs, bKTs, 128)]
            for ri, (bp_, bs_, ci, si, dstT, bdst, r0) in enumerate(rope):
                p1, bp1 = fm_block(bp_)
                p2, bp2 = fm_block(bs_)
                ts_ = ri % 2

                def rope_fn(o, bo, p1=p1, bp1=bp1, p2=p2, bp2=bp2, ci=ci, si=si, ts_=ts_):
                    self.tt("dve", t1[ts_][:, 0:cw], p1[:, 0:cw], tabs[tb][:, ci, 0:cw], ALU.mult, R=[bp1, b_tabs[tb]], W=[b_t1[ts_]], name="rope1")
                    self.tt("dve", t2[ts_][:, 0:cw], p2[:, 0:cw], tabs[tb][:, si, 0:cw], ALU.mult, R=[bp2, b_tabs[tb]], W=[b_t2[ts_]], name="rope2")
                    self.tt("pool", o[:, 0:cw], t1[ts_][:, 0:cw], t2[ts_][:, 0:cw], ALU.add, R=[b_t1[ts_], b_t2[ts_]], W=[bo], name="rope3")
                store_fm(rope_fn, dstT[r0:r0 + 128, cs], bdst)
            for j, blk in enumerate((8, 9)):
                p, bp = fm_block(blk)
                store_fm(lambda o, bo, p=p, bp=bp: self.act(o[:, 0:cw], p[:, 0:cw], AF.Copy, R=[bp], W=[bo], scale=0.125, name="cp_fq"),
                         QTs[256 + j * 128:256 + (j + 1) * 128, cs], bQTs)
            for j, blk in enumerate((10, 11)):
                p, bp = fm_block(blk)
                store_fm(lambda o, bo, p=p, bp=bp: self.cp("dve", o[:, 0:cw], p[:, 0:cw], R=[bp], W=[bo], name="cp_fk"),
                         KTs[256 + j * 128:256 + (j + 1) * 128, cs], bKTs)
            p, bp = next_mm()
            for d in range(8):
                self.mm(p[0:4, 0:cw], Wi[:, d, 1536:1540], hnT[hb][:, d, 0:cw], d == 0, d == 7, R=[b_hnT[hb], b_Wi], PW=[bp], name="mm_ff")
            self.act(ffe[:, 0:cw], p[0:4, 0:cw], AF.Exp, R=[bp, b_fbn], W=[b_ffe], bias=fbn[:, 0:1], scale=-1.0, name="ff_exp")
            self.act(ffn[:, 0:cw], ffe[:, 0:cw], AF.Ln, R=[b_ffe], W=[b_ffn], bias=1.0, scale=1.0, name="ff_ln")
            cb = c % 2
            init = 0.0 if carry is None else carry
            rr = [b_ffn, b_ones4] + ([b_cneg[1 - cb]] if carry is not None else [])
            self.S.op("dve", lambda e, o=cneg[cb][:, 0:cw], d0=ones4[:, 0:cw], d1=ffn[:, 0:cw], ini=init:
                      e.tensor_tensor_scan(out=o, data0=d0, data1=d1, initial=ini, op0=ALU.mult, op1=ALU.add),
                      R=rr, W=[b_cneg[cb]], name="scan")
            carry = cneg[cb][:, cw - 1:cw]
            self.cp("pool", pk[cb][:, 0, 0:cw], cneg[cb][:, 0:cw], R=[b_cneg[cb]], PW=[b_pk[cb]], name="hi")
            self.cp("pool", fft[:, 0:cw], pk[cb][:, 0, 0:cw], R=[b_pk[cb]], W=[b_fft], name="hif")
            self.tt("pool", ffr1[:, 0:cw], cneg[cb][:, 0:cw], fft[:, 0:cw], ALU.subtract, R=[b_cneg[cb], b_fft], W=[b_ffr1], name="r1")
            self.cp("pool", pk[cb][:, 1, 0:cw], ffr1[:, 0:cw], R=[b_ffr1], PW=[b_pk[cb]], name="mid")
            self.cp("pool", fft[:, 0:cw], pk[cb][:, 1, 0:cw], R=[b_pk[cb]], W=[b_fft], name="midf")
            self.tt("pool", ffr2[:, 0:cw], ffr1[:, 0:cw], fft[:, 0:cw], ALU.subtract, R=[b_ffr1, b_fft], W=[b_ffr2], name="r2")
            self.cp("pool", pk[cb][:, 2, 0:cw], ffr2[:, 0:cw], R=[b_ffr2], PW=[b_pk[cb]], name="lo")
            self.ts("pool", pq[cb][:, :, 0:cw], pk[cb][:, :, 0:cw], -1.0, None, ALU.mult, R=[b_pk[cb]], W=[b_pq[cb]], name="negp")
            self.dma("act", CKs[:, :, cs], pk[cb][:, :, 0:cw], b_pk[cb], R=[b_pk[cb]], PW=[bCKs], name="st_ck")
            self.dma("act", CQs[:, :, cs], pq[cb][:, :, 0:cw], b_pq[cb], R=[b_pq[cb]], PW=[bCQs], name="st_cq")

        if not has_layer:
            self.S.final_wait()
            self.S.run()
            return self.nc

        self.S.barrier()
        AR.reset()
        KT = [AR.alloc([LP], BF16) for _ in range(2)]
        b_KT = [self.B("KT") for _ in range(2)]
        Vx = [AR.alloc([NT, 129], BF16) for _ in range(2)]
        b_Vx = [self.B("Vx") for _ in range(2)]
        QTc = [AR.alloc([512], BF16) for _ in range(2)]
        b_QTc = [self.B("QTc") for _ in range(2)]
        Pb = [AR.alloc([512], BF16) for _ in range(3)]
        b_Pb = [self.B("P") for _ in range(3)]
        Pp = [AR.alloc([512], BF16) for _ in range(2)]
        b_Pp = [self.B("Pp") for _ in range(2)]
        gch = [AR.alloc([4, 128], BF16) for _ in range(2)]
        b_gch = [self.B("gch") for _ in range(2)]
        o0_all = AR.alloc([NT, 128], F32)
        b_o0 = self.B("o0all")
        tA = AR.alloc([4, 128], F32)
        tB = AR.alloc([4, 128], F32)
        b_tA, b_tB = self.B("tA"), self.B("tB")
        gout = [AR.alloc([4, 128], BF16) for _ in range(2)]
        b_gout = [self.B("gout") for _ in range(2)]
        rec = AR.alloc([4, 4], F32)
        b_rec = [self.B("rec") for _ in range(4)]
        rs = AR.alloc([4, 4], F32)
        b_rs = [self.B("rs") for _ in range(4)]

        for i in range(2):
            self.memset("pool", KT[i][64:70, :], 1.0, PW=[b_KT[i]], name="ones_kt")
            self.S.seal(b_KT[i])
            self.memset("pool", QTc[i][64:70, :], 1.0, PW=[b_QTc[i]], name="ones_qt")
            self.S.seal(b_QTc[i])
            self.memset("pool", Vx[i][:, :, 128:129], 1.0, PW=[b_Vx[i]], name="ones_v")
            self.S.seal(b_Vx[i])

        S_ps = [ps_all[:, i, :] for i in range(3)]
        b_S = b_bank[0:3]
        Sp_ps = ps_all[:, 3, :]
        b_Sp = b_bank[3]
        O_ps = [ps_all[:, 4:6, :].rearrange("p a (b c) -> p (a b) c", c=256), ps_all[:, 6:8, :].rearrange("p a (b c) -> p (a b) c", c=256)]
        b_O = [b_bank[4], b_bank[6]]

        passes = [("d", 0, 0), ("d", 0, 1), ("d", 1, 0), ("d", 1, 1), ("f", 0, 0), ("f", 1, 0), ("f", 2, 0), ("f", 3, 0)]
        passes = passes[:NPASS]

        def pinfo(pi):
            kind, hidx, m = passes[pi]
            if kind == "d":
                return dict(kind=kind, hidx=hidx, m=m, R=64, dv=128, qrow=hidx * 128 + m * 64, vcol=hidx * 128, gcol=hidx * 128, vlo=0, maskT=maskD,
                            gate=(m == 1), vidx=hidx)
            return dict(kind=kind, hidx=hidx, m=m, R=70, dv=64, qrow=256 + hidx * 64, vcol=256 + hidx * 64, gcol=256 + hidx * 64, vlo=64, maskT=maskF,
                        gate=True, vidx=2 + hidx)

        def pass_loads(pi):
            P = pinfo(pi)
            kb = pi % 2
            self.dma("sp", KT[kb][0:64, :], KTs[P["qrow"]:P["qrow"] + 64, :], b_KT[kb], R=[bKTs], PW=[b_KT[kb]], name=f"ld_KT{pi}")
            if P["kind"] == "f":
                self.dma("sp", KT[kb][67:70, :], CKs[P["hidx"], :, :], b_KT[kb], R=[bCKs], PW=[b_KT[kb]], name=f"ld_CK{pi}")
            if P["m"] == 0:
                vb = P["vidx"] % 2
                dv, vlo, vcol = P["dv"], P["vlo"], P["vcol"]
                self.dma("sp", Vx[vb][:, :, vlo:vlo + dv], Vs[:, :, vcol:vcol + dv], b_Vx[vb], R=[bVs], PW=[b_Vx[vb]], name=f"ld_V{pi}")

        items = [(pi, c) for pi in range(len(passes)) for c in range(NCH)]

        def chunk_loads(ii):
            pi, c = items[ii]
            P = pinfo(pi)
            t0 = 4 * c
            nq = min(4, NT - t0)
            cw = nq * 128
            cs = slice(t0 * 128, t0 * 128 + cw)
            qb = ii % 2
            self.dma("sp", QTc[qb][0:64, 0:cw], QTs[P["qrow"]:P["qrow"] + 64, cs], b_QTc[qb], R=[bQTs], PW=[b_QTc[qb]], name="ld_QT")
            if P["kind"] == "f":
                self.dma("sp", QTc[qb][64:67, 0:cw], CQs[P["hidx"], :, cs], b_QTc[qb], R=[bCQs], PW=[b_QTc[qb]], name="ld_CQ")
            if P["gate"]:
                dv, gcol = P["dv"], P["gcol"]
                self.dma("sp", gch[qb][:, 0:nq, 0:dv], Gs[:, t0:t0 + nq, gcol:gcol + dv], b_gch[qb], R=[bGs], W=[b_gch[qb]], name="ld_gate")

        from collections import deque
        pending = deque()
        LAG = LAG_STEPS

        def push(grp):
            pending.append(grp)
            while len(pending) > LAG:
                for o_ in pending.popleft():
                    self.S.place(o_)

        step_i = 0
        pass_loads(0)
        chunk_loads(0)
        for ii, (pi, c) in enumerate(items):
            P = pinfo(pi)
            kind, m, R_, dv, vlo, gcol, maskT = P["kind"], P["m"], P["R"], P["dv"], P["vlo"], P["gcol"], P["maskT"]
            kb = pi % 2
            vb = P["vidx"] % 2
            if c == 0 and pi + 1 < len(passes):
                pass_loads(pi + 1)
            if ii + 1 < len(items):
                chunk_loads(ii + 1)
            t0 = 4 * c
            nq = min(4, NT - t0)
            cw = nq * 128
            qb = ii % 2
            ob = ii % 2
            gb = qb
            npart = 0
            if kind == "d":
                npart = sum(1 for u in range(nq) if t0 + u + 1 <= NT - 1)
            jmax = t0 + nq - 1
            for j in range(jmax + 1):
                u0 = max(0, j - t0)
                col0 = u0 * 128
                diag = j >= t0
                sbi = step_i % 3
                step_i += 1
                self.mm(S_ps[sbi][:, col0:cw], KT[kb][0:R_, j * 128:(j + 1) * 128], QTc[qb][0:R_, col0:cw], True, not diag,
                        R=[b_KT[kb], b_QTc[qb]], W=[b_S[sbi]], name=f"qk{pi}_{c}_{j}")
                if diag:
                    self.mm(S_ps[sbi][:, col0:col0 + 128], ident[:], maskT[:], False, True, R=[b_ident, b_mask], PW=[b_S[sbi]], name="mask")
                self.act(Pb[sbi][:, col0:cw], S_ps[sbi][:, col0:cw], AF.Exp, R=[b_S[sbi]], W=[b_Pb[sbi]], name=f"exp{pi}_{c}_{j}")
                grp = []
                for u in range(u0, nq):
                    last = (j == t0 + u) and not (kind == "d" and u < npart)
                    grp.append(self.mm(O_ps[ob][:, u, 0:dv + 1], Pb[sbi][:, u * 128:(u + 1) * 128], Vx[vb][:, j, vlo:vlo + dv + 1], (j == 0 and u % 2 == 0), last,
                                       R=[b_Pb[sbi], b_Vx[vb]], PW=[b_O[ob]], place=False, name=f"pv{pi}_{c}_{j}_{u}"))
                push(grp)
            if npart > 0:
                ppb = ii % 2
                pw_ = npart * 128
                self.mm(Sp_ps[0:16, 0:pw_], ident[0:16, 0:16], maskP[0:16, 0:pw_], True, False, R=[b_ident, b_mask], W=[b_Sp], name="pmask")
                for u in range(npart):
                    jn = t0 + u + 1
                    self.mm(Sp_ps[0:16, u * 128:(u + 1) * 128], KT[kb][0:64, jn * 128:jn * 128 + 16], QTc[qb][0:64, u * 128:(u + 1) * 128], False, True,
                            R=[b_KT[kb], b_QTc[qb]], PW=[b_Sp], name="pqk")
                self.act(Pp[ppb][0:16, 0:pw_], Sp_ps[0:16, 0:pw_], AF.Exp, R=[b_Sp], W=[b_Pp[ppb]], name="pexp")
                grp = []
                for u in range(npart):
                    jn = t0 + u + 1
                    grp.append(self.mm(O_ps[ob][:, u, 0:dv + 1], Pp[ppb][0:16, u * 128:(u + 1) * 128], Vx[vb][0:16, jn, 0:dv + 1], False, True,
                                       R=[b_Pp[ppb], b_Vx[vb]], PW=[b_O[ob]], place=False, name="ppv"))
                push(grp)
            ri = ii % 4
            rc = rec[:, ri, 0:nq]
            brc = b_rec[ri]
            Ov = O_ps[ob]
            self.S.op("dve", lambda e, o=rc, i=Ov[:, 0:nq, dv]: e.reciprocal(out=o, in_=i), R=[b_O[ob]], W=[brc], name="recip")
            rcb = rc.rearrange("p (a b) -> p a b", b=1).to_broadcast([128, nq, dv])
            rows_ap = g_out[t0 * 128:t0 * 128 + cw, gcol:gcol + dv].rearrange("(u p) c -> p u c", p=128)
            go = ii % 2
            if kind == "f":
                self.tt("dve", tA[:, 0:nq, 0:dv], Ov[:, 0:nq, 0:dv], rcb, ALU.mult, R=[b_O[ob], brc], W=[b_tA], name="onorm")
                self.tt("pool", gout[go][:, 0:nq, 0:dv], tA[:, 0:nq, 0:dv], gch[gb][:, 0:nq, 0:dv], ALU.mult, R=[b_tA, b_gch[gb]], W=[b_gout[go]], name="ogate")
                self.dma("sp", rows_ap, gout[go][:, 0:nq, 0:dv], b_gout[go], R=[b_gout[go]], name="st_gout")
            elif m == 0:
                self.tt("dve", o0_all[:, t0:t0 + nq, :], Ov[:, 0:nq, 0:dv], rcb, ALU.mult, R=[b_O[ob], brc], PW=[b_o0], name="o0norm")
            else:
                self.tt("dve", tA[:, 0:nq, :], Ov[:, 0:nq, 0:dv], rcb, ALU.mult, R=[b_O[ob], brc], W=[b_tA], name="o1norm")
                self.stt("dve", tB[:, 0:nq, :], tA[:, 0:nq, :], neglam, o0_all[:, t0:t0 + nq, :], ALU.mult, ALU.add, R=[b_tA, b_lamw, b_o0], W=[b_tB], name="odiff")
                self.tt("pool", tA[:, 0:nq, :], tB[:, 0:nq, :], tB[:, 0:nq, :], ALU.mult, R=[b_tB], W=[b_tA], name="osq")
                rsv = rs[:, ri, 0:nq]
                brs = b_rs[ri]
                self.S.op("dve", lambda e, o=rsv, i=tA[:, 0:nq, :]: e.reduce_sum(out=o, in_=i, axis=AX.X), R=[b_tA], W=[brs], name="ossum")
                self.act(rsv, rsv, AF.Sqrt, R=[brs, b_eps], W=[brs], bias=eps5[:, 0:1], scale=1.0 / 128, name="oms")
                self.S.op("dve", lambda e, o=rsv, i=rsv: e.reciprocal(out=o, in_=i), R=[brs], W=[brs], name="orstd")
                rsb = rsv.rearrange("p (a b) -> p a b", b=1).to_broadcast([128, nq, dv])
                self.tt("dve", tB[:, 0:nq, :], tB[:, 0:nq, :], rsb, ALU.mult, R=[b_tB, brs], W=[b_tB], name="osub")
                sgb = sg[:, :].rearrange("p (a b) -> p a b", a=1).to_broadcast([128, nq, dv])
                self.tt("pool", tB[:, 0:nq, :], tB[:, 0:nq, :], sgb, ALU.mult, R=[b_tB, b_sg], W=[b_tB], name="osg")
                self.tt("pool", gout[go][:, 0:nq, 0:dv], tB[:, 0:nq, :], gch[gb][:, 0:nq, 0:dv], ALU.mult, R=[b_tB, b_gch[gb]], W=[b_gout[go]], name="ogate")
                self.dma("sp", rows_ap, gout[go][:, 0:nq, 0:dv], b_gout[go], R=[b_gout[go]], name="st_gout")
        while pending:
            for o_ in pending.popleft():
                self.S.place(o_)
        if DEBUG:
            odump = AR.alloc([4, 512], F32)
            b_odump = self.B("odump")
            self.cp("dve", odump, ps_all[:, 4:8, :], R=[b_bank[4], b_bank[6]], W=[b_odump], name="odump")
            self.dbg("O", odump, b_odump, [128, 4, 512], F32)
            for i_ in range(3):
                self.dbg(f"P{i_}", Pb[i_], b_Pb[i_], [128, 512], BF16)
            for i_ in range(2):
                self.dbg(f"QTc{i_}", QTc[i_], b_QTc[i_], [128, 512], BF16)
                self.dbg(f"KT{i_}", KT[i_], b_KT[i_], [128, LP], BF16)
                self.dbg(f"Vx{i_}", Vx[i_], b_Vx[i_], [128, NT, 129], BF16)
        self.dbg("lamw", lamw[:], b_lamw, [128, 8], F32)
        self.dbg("sg", sg[:], b_sg, [128, 128], F32)
        self.dbg("o0", o0_all, b_o0, [128, NT, 128], F32)
        self.dbg("tA", tA, b_tA, [128, 4, 128], F32)
        self.dbg("tB", tB, b_tB, [128, 4, 128], F32)
        self.dbg("rec", rec, b_rec[0], [128, 4, 4], F32)
        self.dbg("rs", rs, b_rs[0], [128, 4, 4], F32)
        self.dbg_flush()
        self.S.final_wait()
        self.S.run()
        return self.nc


_BF = ml_dtypes.bfloat16
_PROGS = {}


def _prog(NT, mode):
    key = (NT, mode)
    if key not in _PROGS:
        _PROGS[key] = Builder(NT, mode).build()
    return _PROGS[key]


def _consts(LP):
    ident = np.eye(128, dtype=np.float32).astype(_BF)
    k = np.arange(128)[:, None]
    q = np.arange(128)[None, :]
    maskF = np.where(k <= q, 0.0, NEG).astype(np.float32).astype(_BF)

    def cidx(a):
        return (a >= 16).astype(np.int32) + (a >= 80).astype(np.int32)
    maskD = np.where(cidx(k) <= cidx(q), 0.0, NEG).astype(np.float32).astype(_BF)
    b = np.arange(512) % 128
    maskP = np.broadcast_to(np.where(b >= 80, 0.0, NEG)[None, :], (16, 512)).astype(np.float32).astype(_BF)
    pos = np.arange(LP, dtype=np.float32)
    inv_freq = (1.0 / (np.float32(500000.0) ** (np.arange(0, 16, 2, dtype=np.float32) / np.float32(16)))).astype(np.float32)
    ang = (pos[:, None] * inv_freq[None, :]).astype(np.float32)
    cos = np.cos(ang).astype(np.float32).T
    sin = np.sin(ang).astype(np.float32).T
    C = np.concatenate([cos, cos], 0)
    Sg = np.concatenate([-sin, sin], 0)
    tab = np.stack([0.125 * C, 0.125 * Sg, C, Sg], axis=1).astype(np.float32)
    return dict(ident=ident, maskF=maskF, maskD=maskD, maskP=np.ascontiguousarray(maskP), tab=np.ascontiguousarray(tab))


def _wi_cols(g):
    hA, hB = 2 * g, 2 * g + 1
    ar = np.arange

    def swap(cols):
        c = cols.reshape(2, 64)
        sw = c.copy()
        sw[:, 0:8] = c[:, 8:16]
        sw[:, 8:16] = c[:, 0:8]
        return sw.reshape(-1)
    dq = [0 + h * 128 + ar(128) for h in (hA, hB)]
    dk = [512 + h * 128 + ar(128) for h in (hA, hB)]
    fq = 2048 + 4 * g * 64 + ar(256)
    fk = 2560 + 4 * g * 64 + ar(256)
    ff = 3584 + 4 * g + ar(4)
    V = [1024 + hA * 128 + ar(128), 1024 + hB * 128 + ar(128), 3072 + 4 * g * 64 + ar(256)]
    G = [1536 + hA * 128 + ar(128), 1536 + hB * 128 + ar(128), 3592 + 4 * g * 64 + ar(256)]
    cols = [dq[0], dq[1], swap(dq[0]), swap(dq[1]), dk[0], dk[1], swap(dk[0]), swap(dk[1]), fq[:128], fq[128:], fk[:128], fk[128:], ff] + V + G
    cols = np.concatenate(cols)
    assert cols.shape[0] == WCOLS
    return cols


_WO_PERM = np.concatenate([np.arange(0, 256), np.arange(512, 768), np.arange(256, 512), np.arange(768, 1024)])


def _layer_inputs(l, g, P, consts):
    lam_init = 0.8 - 0.6 * math.exp(-0.3 * l)
    d = dict(
        wi=np.ascontiguousarray(P["w_in"][l][:, _wi_cols(g)]),
        gain=np.ascontiguousarray(np.broadcast_to(P["norm_gain"][l][None, :], (128, D))),
        fb=np.ascontiguousarray(P["forget_bias"][l][4 * g:4 * g + 4].reshape(4, 1)),
        tab=consts["tab"],
        lam4=np.ascontiguousarray(np.broadcast_to(np.stack([P["lambda_q1"][l], P["lambda_k1"][l], P["lambda_q2"][l], P["lambda_k2"][l]])[None], (128, 4, 64))),
        sg=np.ascontiguousarray(np.broadcast_to(P["subln_gain"][l][None, :], (128, 128))),
        linit=np.ascontiguousarray(np.broadcast_to(np.array([lam_init, 1.0 - lam_init], np.float32)[None], (128, 2))),
        maskF=consts["maskF"], maskD=consts["maskD"], maskP=consts["maskP"], ident=consts["ident"],
    )
    return d


def run_module(x, P, NT):
    Bn, seq, _ = x.shape
    LP = NT * 128
    assert NMETA + seq <= LP
    consts = _consts(LP)
    h = np.zeros((Bn, LP, D), np.float32)
    h[:, :NMETA] = P["meta_tokens"][None]
    h[:, NMETA:NMETA + seq] = x
    ncores = 2 * Bn
    cores = list(range(ncores))
    gcat = None
    depth = P["w_in"].shape[0]
    for l in range(depth):
        mode = "first" if l == 0 else "mid"
        nc = _prog(NT, mode)
        in_maps = []
        for i in cores:
            b, g = i // 2, i % 2
            m = _layer_inputs(l, g, P, consts)
            m["h_in"] = h[b]
            if l > 0:
                m["gcat"] = gcat[b]
                m["wo"] = np.ascontiguousarray(P["w_out"][l - 1][_WO_PERM, :])
            in_maps.append(m)
        res = run_bass_kernel_spmd(nc, in_maps, core_ids=cores).results
        if l > 0:
            h = np.stack([res[2 * b]["h_out"] for b in range(Bn)])
        gcat = [np.concatenate([res[2 * b]["g_out"], res[2 * b + 1]["g_out"]], axis=1) for b in range(Bn)]
    NTF = (NT + 1) // 2
    r0 = [0, (NT - NTF) * 128]
    nc = _prog(NTF, "final")
    in_maps = []
    for i in cores:
        b, g = i // 2, i % 2
        rs = slice(r0[g], r0[g] + NTF * 128)
        in_maps.append(dict(h_in=np.ascontiguousarray(h[b][rs]), gcat=np.ascontiguousarray(gcat[b][rs]),
                            wo=np.ascontiguousarray(P["w_out"][depth - 1][_WO_PERM, :]),
                            gain=np.ascontiguousarray(np.broadcast_to(P["final_gain"][None, :], (128, D))), ident=consts["ident"]))
    res = run_bass_kernel_spmd(nc, in_maps, core_ids=cores).results
    out = np.zeros((Bn, LP, D), np.float32)
    for b in range(Bn):
        out[b, r0[1]:r0[1] + NTF * 128] = res[2 * b + 1]["y_out"]
        out[b, 0:NTF * 128] = res[2 * b]["y_out"]
    return np.ascontiguousarray(out[:, NMETA:NMETA + seq])


def kernel(x, meta_tokens, norm_gain, w_in, forget_bias, lambda_q1, lambda_k1, lambda_q2, lambda_k2, subln_gain, w_out, final_gain):
    P = dict(meta_tokens=meta_tokens, norm_gain=norm_gain, w_in=w_in, forget_bias=forget_bias, lambda_q1=lambda_q1, lambda_k1=lambda_k1,
             lambda_q2=lambda_q2, lambda_k2=lambda_k2, subln_gain=subln_gain, w_out=w_out, final_gain=final_gain)
    P = {k: np.asarray(v, dtype=np.float32) for k, v in P.items()}
    x = np.asarray(x, dtype=np.float32)
    seq = x.shape[1]
    NT = (NMETA + seq + 127) // 128
    return run_module(x, P, NT)
```
